# Optimizing a Trainium2 kernel written in Bass

```python
import math
import jax, jax.numpy as jnp
from jax import lax
import numpy as np

D_MODEL = 1024
BATCH = 32
SEQ = 2048
DEPTH = 2

N_MIXERS = 2
N_A_LAYERS = (DEPTH + 1) // 2
N_B_LAYERS = DEPTH // 2
BRANCH = D_MODEL
CONFORMER_K = 31
SHORT_K = 3
ALPHA = (2.0 * DEPTH) ** 0.25
BETA = (8.0 * DEPTH) ** -0.25
LN_EPS = 1e-5

kernel_name = "hybrid_conformer_shortconv_deepnorm_adaln"


def _layer_norm(x, g, b):
    xf = x.astype(jnp.float32)
    mu = jnp.mean(xf, axis=-1, keepdims=True)
    var = jnp.mean(jnp.square(xf - mu), axis=-1, keepdims=True)
    y = (xf - mu) * lax.rsqrt(var + LN_EPS) * g.astype(jnp.float32) + b.astype(jnp.float32)
    return y.astype(x.dtype)


def _causal_depthwise_conv(h, w):
    k = w.shape[0]
    return lax.conv_general_dilated(
        h, w[:, None, :].astype(h.dtype),
        window_strides=(1,), padding=[(k - 1, 0)],
        dimension_numbers=("NWC", "WIO", "NWC"),
        feature_group_count=h.shape[-1])


def _conformer_mixer(u, w_in, b_in, conv_w, conv_b, ln_g, ln_b, w_out, b_out):
    p = jnp.einsum("bsd,de->bse", u, w_in) + b_in
    a, g, z = jnp.split(p, 3, axis=-1)
    h = a * jax.nn.sigmoid(g)
    h = _causal_depthwise_conv(h, conv_w) + conv_b
    h = jax.nn.silu(_layer_norm(h, ln_g, ln_b))
    return jnp.einsum("bse,ed->bsd", h * jax.nn.silu(z), w_out) + b_out


def _short_conv_mixer(u, w_in, conv_w, w_out):
    p = jnp.einsum("bsd,de->bse", u, w_in)
    bg, cg, v, z = jnp.split(p, 4, axis=-1)
    h = _causal_depthwise_conv(cg * v, conv_w)
    return jnp.einsum("bse,ed->bsd", bg * h * jax.nn.silu(z), w_out)


def setup_inputs(seed: int = 0) -> dict:
    key = jax.random.key(seed)
    ks = jax.random.split(key, 20)
    d, e = D_MODEL, BRANCH
    f32 = jnp.float32
    nrm = lambda k, shape, s: jax.random.normal(k, shape, f32) * s
    return {
        "x": nrm(ks[0], (BATCH, SEQ, d), 1.0),
        "c": nrm(ks[1], (BATCH, d), 1.0),
        "ada_w": nrm(ks[2], (DEPTH, d, 3 * d), 0.5 * d ** -0.5),
        "ada_b": nrm(ks[3], (DEPTH, 3 * d), 0.01),
        "a_w_in": nrm(ks[4], (N_A_LAYERS, d, 3 * e), d ** -0.5),
        "a_b_in": nrm(ks[5], (N_A_LAYERS, 3 * e), 0.01),
        "a_conv_w": nrm(ks[6], (N_A_LAYERS, CONFORMER_K, e), CONFORMER_K ** -0.5),
        "a_conv_b": nrm(ks[7], (N_A_LAYERS, e), 0.01),
        "a_ln_g": 1.0 + nrm(ks[8], (N_A_LAYERS, e), 0.02),
        "a_ln_b": nrm(ks[9], (N_A_LAYERS, e), 0.01),
        "a_w_out": nrm(ks[10], (N_A_LAYERS, e, d), BETA * e ** -0.5),
        "a_b_out": nrm(ks[11], (N_A_LAYERS, d), 0.01),
        "b_w_in": nrm(ks[12], (N_B_LAYERS, d, 4 * e), d ** -0.5),
        "b_conv_w": nrm(ks[13], (N_B_LAYERS, SHORT_K, e), SHORT_K ** -0.5),
        "b_w_out": nrm(ks[14], (N_B_LAYERS, e, d), BETA * e ** -0.5),
        "post_ln_g": 1.0 + nrm(ks[15], (DEPTH, d), 0.02),
        "post_ln_b": nrm(ks[16], (DEPTH, d), 0.01),
    }


def reference(x, c, ada_w, ada_b, a_w_in, a_b_in, a_conv_w, a_conv_b, a_ln_g, a_ln_b,
              a_w_out, a_b_out, b_w_in, b_conv_w, b_w_out, post_ln_g, post_ln_b):
    cond = jax.nn.silu(c)
    for l in range(DEPTH):
        mod = jnp.einsum("bd,de->be", cond, ada_w[l]) + ada_b[l]
        shift, scale, gate = jnp.split(mod[:, None, :], 3, axis=-1)
        u = x * (1.0 + scale) + shift
        i = l // N_MIXERS
        if l % N_MIXERS == 0:
            y = _conformer_mixer(u, a_w_in[i], a_b_in[i], a_conv_w[i], a_conv_b[i],
                                 a_ln_g[i], a_ln_b[i], a_w_out[i], a_b_out[i])
        else:
            y = _short_conv_mixer(u, b_w_in[i], b_conv_w[i], b_w_out[i])
        x = _layer_norm(ALPHA * x + gate * y, post_ln_g[l], post_ln_b[l])
    return x
```

```python
import numpy as np
from contextlib import ExitStack
import concourse.bass as bass
import concourse.mybir as mybir
from concourse.bass_utils import run_bass_kernel_spmd

F32 = mybir.dt.float32
BF16 = mybir.dt.bfloat16
AF = mybir.ActivationFunctionType
ALU = mybir.AluOpType

D = 1024
KC = 8
TOK = 512
NB = TOK // 128
K0 = 31
K1 = 3
DEPTH = 2
ALPHA = (2.0 * DEPTH) ** 0.25
LN_EPS = 1e-5
EPS_POST = LN_EPS / (ALPHA * ALPHA)

COMPUTE = ("pe", "act", "dve", "pool")

V_BIN0 = 0
V_CW0 = 24
V_CB0 = V_CW0 + 8 * K0
V_LNG = V_CB0 + 8
V_LNB = V_LNG + 8
V_CW1 = V_LNB + 8
NV = V_CW1 + 8 * K1


class Buf:
    __slots__ = ("name", "last_w", "readers")

    def __init__(self, name):
        self.name = name
        self.last_w = None
        self.readers = []


class Op:
    __slots__ = ("eng", "fn", "waits", "sig", "dma")


class Prog:
    def __init__(self, nc):
        self.nc = nc
        self.ops = {e: [] for e in COMPUTE + ("sp",)}
        self.clock = {e: {} for e in COMPUTE + ("sp",)}
        self.nops = {}
        self.clk_of = {}
        self.needed = set()
        self.dma_groups = []

    def _dep(self, eng, sig, waits):
        actor, idx = sig
        ck = self.clock[eng]
        if ck.get(actor, 0) >= idx:
            return
        waits.append(sig)
        self.needed.add(sig)
        for a, v in self.clk_of[sig].items():
            if ck.get(a, 0) < v:
                ck[a] = v
        if ck.get(actor, 0) < idx:
            ck[actor] = idx

    def op(self, eng, fn, reads=(), writes=(), dma=None):
        waits = []
        for b in reads:
            s = b.last_w
            if s is not None:
                if s[0] == eng and eng == "pe":
                    continue
                self._dep(eng, s, waits)
        for b in writes:
            s = b.last_w
            if s is not None and not (s[0] == eng and dma is None):
                self._dep(eng, s, waits)
            for s in b.readers:
                if not (s[0] == eng and dma is None):
                    self._dep(eng, s, waits)
        actor = dma if dma is not None else eng
        if dma is not None:
            if dma not in self.nops:
                self.dma_groups.append(dma)
            elif self.nops[dma] > 0:
                self._dep(eng, (dma, self.nops[dma]), waits)
        idx = self.nops.get(actor, 0) + 1
        self.nops[actor] = idx
        sig = (actor, idx)
        o = Op()
        o.eng, o.fn, o.waits, o.sig, o.dma = eng, fn, waits, sig, dma
        snap = dict(self.clock[eng])
        snap[actor] = idx
        self.clk_of[sig] = snap
        self.ops[eng].append(o)
        for b in writes:
            b.last_w = sig
            b.readers = []
        for b in reads:
            if b not in writes:
                b.readers.append(sig)
        return sig

    def fence(self, eng, sigs):
        waits = []
        for s in sigs:
            self._dep(eng, s, waits)
        o = Op()
        o.eng, o.fn, o.waits, o.sig, o.dma = eng, None, waits, None, None
        self.ops[eng].append(o)

    def emit(self):
        nc = self.nc
        sems = {}
        for e in COMPUTE:
            sems[e] = nc.alloc_semaphore("s_" + e)
        for g in self.dma_groups:
            sems[g] = nc.alloc_semaphore("d_" + g)
        semval = {}
        for e in COMPUTE:
            v = 0
            for o in self.ops[e]:
                if o.sig is not None and o.dma is None and o.sig in self.needed:
                    v += 1
                    semval[o.sig] = v

        def val(sig):
            if sig[0] in COMPUTE:
                return semval[sig]
            return 16 * sig[1]

        def run(ename):
            def body(e):
                for o in self.ops[ename]:
                    for s in o.waits:
                        e.wait_ge(sems[s[0]], val(s))
                    if o.fn is None:
                        continue
                    ins = o.fn(e)
                    if o.dma is not None:
                        ins.then_inc(sems[o.dma], 16)
                    elif o.sig in self.needed:
                        ins.then_inc(sems[ename], 1)
            return body

        with nc.Block() as block:
            block.tensor(run("pe"))
            block.scalar(run("act"))
            block.vector(run("dve"))
            block.gpsimd(run("pool"))
            block.sync(run("sp"))


class Ring:
    def __init__(self, items):
        self.items = items
        self.i = 0

    def next(self):
        it = self.items[self.i % len(self.items)]
        self.i += 1
        return it


def build(nseq, seq, layers=(0, 1), dbg=None):
    ntok = nseq * seq
    tps = seq // TOK
    nc = bass.Bass("TRN2", target_bir_lowering=False)

    def din(name, shape, dt=F32):
        return nc.dram_tensor(name, list(shape), dt, kind="ExternalInput").ap()

    x_d = din("x", [ntok, D])
    cT_d = din("cT", [128, KC, nseq])
    adaw_d = din("ada_w", [2, D, 3 * D])
    adab_d = din("ada_b4", [2, nseq, 3 * D])
    win_d = [din("win0", [24 * 128, 1024]), din("win1", [32 * 128, 1024])]
    wout_d = [din("wout0", [128, KC, D]), din("wout1", [128, KC, D])]
    vec_d = din("vecs", [128, NV])
    bout_d = din("bout0", [1, D])
    pg_d = din("pg", [2, 128, D])
    pb_d = din("pb", [2, 128, D])
    ident_d = din("ident", [128, 128])
    sel_d = din("sel", [nseq, nseq, 128])
    out_d = nc.dram_tensor("out", [ntok, D], F32, kind="ExternalOutput").ap()
    ws_d = [nc.dram_tensor("ws0", [24 * 128, 1024], BF16, kind="Internal").ap(),
            nc.dram_tensor("ws1", [32 * 128, 1024], BF16, kind="Internal").ap()]

    P = Prog(nc)
    es = ExitStack()

    def sb(name, shape, dt=F32):
        return es.enter_context(nc.sbuf_tensor("sb_" + name, list(shape), dt))

    def ps(name, shape, dt=F32):
        return es.enter_context(nc.psum_tensor("pp_" + name, list(shape), dt))

    with es:
        XR = [sb(f"xr{i}", [128, NB, D]) for i in range(2)]
        XRb = [[Buf(f"xr{i}_{k}") for k in range(NB)] for i in range(2)]
        UT = sb("ut", [128, KC, TOK], BF16)
        UTb = [Buf(f"ut{k}") for k in range(KC)]
        H = sb("h", [128, KC, K0 - 1 + TOK], BF16)
        Hb = [Buf(f"h{k}") for k in range(KC)]
        CV = sb("cv", [128, KC, K1 - 1 + TOK], BF16)
        CVb = [Buf(f"cv{k}") for k in range(KC)]
        HC = sb("hc", [128, KC, TOK])
        HCb = [Buf(f"hc{k}") for k in range(KC)]
        HZ = sb("hz", [128, KC, TOK], BF16)
        HZb = [Buf(f"hz{k}") for k in range(KC)]
        DGt = sb("dg", [128, 2, K0, 128], BF16)
        DG = Ring([(DGt[:, i], Buf(f"dg{i}")) for i in range(2)])
        DG1 = sb("dg1", [128, KC, K1, 128], BF16)
        DG1b = Buf("dg1")
        WO = [sb(f"wo{l}", [128, KC, D], BF16) for l in range(2)]
        WOb = [Buf(f"wo{l}") for l in range(2)]
        NSLOT = 6
        WRt = sb("wr", [128, NSLOT, KC, 128], BF16)
        WR = Ring([(WRt[:, i], Buf(f"wr{i}"), f"wr{i}") for i in range(NSLOT)])
        G = [sb(f"g{l}", [128, D]) for l in range(2)]
        Gb = [Buf(f"g{l}") for l in range(2)]
        PG = [sb(f"pg{l}", [128, D]) for l in range(2)]
        PB = [sb(f"pb{l}", [128, D]) for l in range(2)]
        PGb = Buf("pgpb")
        NT32 = 5
        T32t = sb("t32", [128, NT32, TOK])
        T32 = Ring([(T32t[:, i], Buf(f"t32_{i}")) for i in range(NT32)])
        NT16 = 4
        T16t = sb("t16", [128, NT16, TOK], BF16)
        T16 = Ring([(T16t[:, i], Buf(f"t16_{i}")) for i in range(NT16)])
        MEAN = sb("mean", [128, TOK]); MEANb = Buf("mean")
        MSQ = sb("msq", [128, TOK]); MSQb = Buf("msq")
        RSTD = sb("rstd", [128, TOK]); RSTDb = Buf("rstd")
        GATE = [sb(f"gate{l}", [nseq, D]) for l in range(2)]
        GATEb = [Buf(f"gate{l}") for l in range(2)]
        SS = [sb(f"ss{l}", [128, 2, KC, nseq]) for l in range(2)]
        SSb = [Buf(f"ss{l}") for l in range(2)]
        VEC = sb("vec", [128, NV]); VECb = Buf("vec")
        IDF = sb("idf", [128, 128]); IDFb = Buf("idf")
        IDB = sb("idb", [128, 128], BF16); IDBb = Buf("idb")
        ONESB = sb("onesb", [128, 128], BF16); ONESBb = Buf("onesb")
        ONER = sb("oner", [1, 128], BF16); ONERb = Buf("oner")
        BOUT = sb("bout", [1, D], BF16); BOUTb = Buf("bout")
        SEL = sb("sel", [nseq, nseq, 128]); SELb = Buf("sel")
        CT = sb("ct", [128, KC, nseq]); CTb = Buf("ct")
        EPSC = sb("epsc", [128, 2]); EPSCb = Buf("epsc")
        NSM = 4
        STt = sb("st", [128, NSM, 12]); MVt = sb("mv", [128, NSM, 2]); RSt = sb("rs", [128, NSM, 1])
        SM = Ring([(STt[:, i], MVt[:, i], RSt[:, i], Buf(f"sm{i}")) for i in range(NSM)])
        CW0 = VEC[:, V_CW0:V_CW0 + 8 * K0].rearrange("p (j k) -> p j k", k=K0)
        CW1 = VEC[:, V_CW1:V_CW1 + 8 * K1].rearrange("p (j k) -> p j k", k=K1)

        PSt = [ps(f"ps{i}", [128, TOK]) for i in range(8)]
        PSR = Ring([(PSt[i], Buf(f"ps{i}")) for i in range(6)])
        PSUMS, PSUMSb = PSt[6], Buf("ps6")
        PSSQ, PSSQb = PSt[7], Buf("ps7")
        WSb = [[Buf(f"ws0_{g}") for g in range(6)], [Buf(f"ws1_{g}") for g in range(8)]]

        P.op("sp", lambda e: e.dma_start(out=VEC[:], in_=vec_d), writes=[VECb], dma="c0")
        P.op("sp", lambda e: e.dma_start(out=IDF[:], in_=ident_d), writes=[IDFb], dma="c2")
        P.op("sp", lambda e: e.dma_start(out=SEL[:], in_=sel_d), writes=[SELb], dma="c3")
        P.op("sp", lambda e: e.dma_start(out=CT[:], in_=cT_d), writes=[CTb], dma="c4")
        for l in range(2):
            P.op("sp", lambda e, l=l: e.dma_start(out=PG[l][:], in_=pg_d[l]), writes=[PGb], dma="c1")
            P.op("sp", lambda e, l=l: e.dma_start(out=PB[l][:], in_=pb_d[l]), writes=[PGb], dma="c1")
        P.op("dve", lambda e: e.tensor_copy(out=IDB[:], in_=IDF[:]), reads=[IDFb], writes=[IDBb])
        P.op("dve", lambda e: e.memset(ONESB[:], 1.0 / 1024.0), writes=[ONESBb])
        P.op("dve", lambda e: e.memset(ONER[:], 1.0), writes=[ONERb])
        P.op("dve", lambda e: e.memset(EPSC[:, 0:1], LN_EPS), writes=[EPSCb])
        P.op("dve", lambda e: e.memset(EPSC[:, 1:2], EPS_POST), writes=[EPSCb])
        if 1 in layers:
            for j in range(KC):
                P.op("pool", lambda e, j=j: e.tensor_tensor(
                    out=DG1[:, j], in0=IDB[:].unsqueeze(1).to_broadcast([128, K1, 128]),
                    in1=CW1[:, j, :].unsqueeze(2).to_broadcast([128, K1, 128]), op=ALU.mult),
                    reads=[IDBb, VECb], writes=[DG1b])

        P.op("act", lambda e: e.activation(out=CT[:], in_=CT[:], func=AF.Silu), reads=[CTb], writes=[CTb])
        MODv = HC[0:nseq].rearrange("p k t -> p (k t)")[:, 0:3 * D]
        AWv = [XR[i][:].rearrange("p b d -> p (b d)")[:, 0:3 * D] for i in range(2)]
        for l in layers:
            P.op("sp", lambda e, l=l: e.dma_start(out=MODv, in_=adab_d[l]), writes=HCb, dma="ab")
            banks = [PSR.next() for _ in range(6)]
            for kc in range(KC):
                i = kc % 2
                P.op("sp", lambda e, l=l, kc=kc, i=i: e.dma_start(
                    out=AWv[i], in_=adaw_d[l, kc * 128:(kc + 1) * 128, :]), writes=XRb[i], dma=f"aw{i}")
                for cb in range(6):
                    P.op("pe", lambda e, kc=kc, cb=cb, i=i, bk=banks[cb][0]: e.matmul(
                        bk[0:nseq, :], lhsT=CT[:, kc, :], rhs=AWv[i][:, cb * 512:(cb + 1) * 512],
                        start=(kc == 0), stop=(kc == KC - 1)),
                        reads=[CTb] + XRb[i], writes=[banks[cb][1]])
            for cb in range(6):
                P.op("dve", lambda e, cb=cb, bk=banks[cb][0]: e.tensor_tensor(
                    out=MODv[:, cb * 512:(cb + 1) * 512], in0=bk[0:nseq, :],
                    in1=MODv[:, cb * 512:(cb + 1) * 512], op=ALU.add),
                    reads=[banks[cb][1]] + HCb, writes=HCb)
            P.op("dve", lambda e: e.tensor_scalar(out=MODv[:, D:2 * D], in0=MODv[:, D:2 * D], scalar1=1.0,
                                                  scalar2=None, op0=ALU.add), reads=HCb, writes=HCb)
            P.op("dve", lambda e, l=l: e.tensor_copy(out=GATE[l][:], in_=MODv[:, 2 * D:3 * D]),
                 reads=HCb, writes=[GATEb[l]])
            tp, tpb = PSR.next()
            for s in range(2):
                for kc in range(KC):
                    c0 = s * D + kc * 128
                    o0 = (s * KC + kc) * nseq
                    P.op("pe", lambda e, c0=c0, o0=o0, tp=tp: e.transpose(
                        out=tp[:, o0:o0 + nseq], in_=MODv[:, c0:c0 + 128], identity=IDF[0:nseq, 0:nseq]),
                        reads=HCb + [IDFb], writes=[tpb])
            P.op("dve", lambda e, l=l, tp=tp: e.tensor_copy(
                out=SS[l][:].rearrange("p s k b -> p (s k b)"), in_=tp[:, 0:2 * KC * nseq]),
                reads=[tpb], writes=[SSb[l]])

        P.op("sp", lambda e: e.dma_start(out=G[0][0:1, :], in_=bout_d), writes=[Gb[0]], dma="cb")
        P.op("dve", lambda e: e.tensor_copy(out=BOUT[:], in_=G[0][0:1, :]), reads=[Gb[0]], writes=[BOUTb])
        BFS = [UT[:].rearrange("p k t -> p (k t)").rearrange("p (u c) -> p u c", u=4),
               HZ[:].rearrange("p k t -> p (k t)").rearrange("p (u c) -> p u c", u=4)]
        BFSb = [UTb, HZb]
        cast_eng = ["dve", "act", "pool"]
        cnt = [0]

        def convert(src_ap, dst_sb=None, dst_sb_bufs=None, dst_dram=None, dst_dram_buf=None):
            k = cnt[0]
            cnt[0] += 1
            i = k % 2
            eng = cast_eng[k % 3]
            P.op("sp", lambda e, i=i, src_ap=src_ap: e.dma_start(out=XR[i][:], in_=src_ap), writes=XRb[i], dma=f"cl{i}")
            if dst_sb is not None:
                o_ap, o_bufs = dst_sb, dst_sb_bufs
            else:
                o_ap, o_bufs = BFS[i], BFSb[i]
            if eng == "act":
                P.op("act", lambda e, i=i, o_ap=o_ap: e.activation(out=o_ap, in_=XR[i][:], func=AF.Copy),
                     reads=XRb[i], writes=o_bufs)
            else:
                P.op(eng, lambda e, i=i, o_ap=o_ap: e.tensor_copy(out=o_ap, in_=XR[i][:]), reads=XRb[i], writes=o_bufs)
            if dst_dram is not None:
                P.op("sp", lambda e, i=i, dst_dram=dst_dram: e.dma_start(out=dst_dram, in_=BFS[i]),
                     reads=BFSb[i], writes=[dst_dram_buf], dma=f"cs{i}")

        for l in layers:
            for g in range(6 if l == 0 else 8):
                r0 = g * 512
                convert(win_d[l][r0:r0 + 512, :].rearrange("(u p) c -> p u c", p=128),
                        dst_dram=ws_d[l][r0:r0 + 512, :].rearrange("(u p) c -> p u c", p=128),
                        dst_dram_buf=WSb[l][g])
            for h in range(2):
                convert(wout_d[l][:, 4 * h:4 * h + 4, :], dst_sb=WO[l][:, 4 * h:4 * h + 4, :], dst_sb_bufs=[WOb[l]])

        def wload(l, u):
            ap, b, grp = WR.next()
            P.op("sp", lambda e, ap=ap, l=l, u=u: e.dma_start(
                out=ap.rearrange("p k m -> p (k m)"), in_=ws_d[l][u * 128:(u + 1) * 128, :]),
                reads=[WSb[l][u // 4]], writes=[b], dma=grp)
            return ap, b

        def mm_in(w, wb, pt, ptb):
            for kc in range(KC):
                P.op("pe", lambda e, kc=kc, w=w, pt=pt: e.matmul(
                    pt[:, :], lhsT=w[:, kc, :], rhs=UT[:, kc, :], start=(kc == 0), stop=(kc == KC - 1)),
                    reads=[wb, UTb[kc]], writes=[ptb])

        def load_x(t, i):
            r0 = t * TOK
            P.op("sp", lambda e, r0=r0, i=i: e.dma_start(
                out=XR[i][:], in_=x_d[r0:r0 + TOK, :].rearrange("(b p) d -> p b d", p=128)),
                writes=XRb[i], dma=f"xl{i}")

        def front(l, b, i):
            for kc in range(KC):
                tp, tpb = PSR.next()
                for blk in range(NB):
                    P.op("pe", lambda e, kc=kc, blk=blk, tp=tp: e.transpose(
                        out=tp[:, blk * 128:(blk + 1) * 128], in_=XR[i][:, blk, kc * 128:(kc + 1) * 128],
                        identity=IDF[:]), reads=[XRb[i][blk], IDFb], writes=[tpb])
                P.op("act", lambda e, kc=kc, tp=tp: e.activation(
                    out=UT[:, kc, :], in_=tp[:, :], func=AF.Identity,
                    scale=SS[l][:, 1, kc, b:b + 1], bias=SS[l][:, 0, kc, b:b + 1]),
                    reads=[tpb, SSb[l]], writes=[UTb[kc]])

        def conv0(j, first, last):
            dg, dgb = DG.next()
            P.op("pool", lambda e, j=j, dg=dg: e.tensor_tensor(
                out=dg, in0=IDB[:].unsqueeze(1).to_broadcast([128, K0, 128]),
                in1=CW0[:, j, :].unsqueeze(2).to_broadcast([128, K0, 128]), op=ALU.mult),
                reads=[IDBb, VECb], writes=[dgb])
            pc, pcb = PSR.next()
            for k in range(K0):
                P.op("pe", lambda e, j=j, k=k, dg=dg, pc=pc: e.matmul(
                    pc[:, :], lhsT=dg[:, k, :], rhs=H[:, j, k:k + TOK], start=(k == 0), stop=(k == K0 - 1)),
                    reads=[dgb, Hb[j]], writes=[pcb])
            cb = VEC[:, V_CB0 + j:V_CB0 + j + 1]
            P.op("act", lambda e, j=j, pc=pc, cb=cb: e.activation(
                out=HC[:, j, :], in_=pc[:, :], func=AF.Identity, bias=cb, scale=1.0),
                reads=[pcb, VECb], writes=[HCb[j]])
            sq, sqb = T16.next()
            P.op("act", lambda e, pc=pc, cb=cb, sq=sq: e.activation(
                out=sq, in_=pc[:, :], func=AF.Square, bias=cb, scale=1.0),
                reads=[pcb, VECb], writes=[sqb])
            hb, hbb = T16.next()
            P.op("pool", lambda e, j=j, hb=hb: e.tensor_copy(out=hb, in_=HC[:, j, :]),
                 reads=[HCb[j]], writes=[hbb])
            P.op("pe", lambda e, j=j, hb=hb: e.matmul(PSUMS[:, :], lhsT=ONESB[:], rhs=hb,
                                                      start=(j == 0), stop=(j == KC - 1)),
                 reads=[ONESBb, hbb], writes=[PSUMSb])
            P.op("pe", lambda e, j=j, sq=sq: e.matmul(PSSQ[:, :], lhsT=ONESB[:], rhs=sq,
                                                      start=(j == 0), stop=(j == KC - 1)),
                 reads=[ONESBb, sqb], writes=[PSSQb])
            if not last:
                P.op("pool", lambda e, j=j: e.tensor_copy(out=H[:, j, 0:K0 - 1], in_=H[:, j, TOK:TOK + K0 - 1]),
                     reads=[Hb[j]], writes=[Hb[j]])

        def mixer_a(first, last):
            if first:
                for j in range(KC):
                    P.op("pool", lambda e, j=j: e.memset(H[:, j, 0:K0 - 1], 0.0), writes=[Hb[j]])
            for j in range(KC):
                wa, wab = wload(0, j)
                wg, wgb = wload(0, 8 + j)
                pa, pab = PSR.next()
                pg_, pgb = PSR.next()
                mm_in(wa, wab, pa, pab)
                mm_in(wg, wgb, pg_, pgb)
                sg, sgb = T32.next()
                P.op("act", lambda e, j=j, pg_=pg_, sg=sg: e.activation(
                    out=sg, in_=pg_[:, :], func=AF.Sigmoid, bias=VEC[:, V_BIN0 + 8 + j:V_BIN0 + 9 + j], scale=1.0),
                    reads=[pgb, VECb], writes=[sgb])
                P.op("dve", lambda e, j=j, pa=pa, sg=sg: e.scalar_tensor_tensor(
                    out=H[:, j, K0 - 1:K0 - 1 + TOK], in0=pa[:, :], scalar=VEC[:, V_BIN0 + j:V_BIN0 + j + 1],
                    in1=sg, op0=ALU.add, op1=ALU.mult), reads=[pab, sgb, VECb], writes=[Hb[j]])
                if j >= 1:
                    conv0(j - 1, first, last)
            conv0(KC - 1, first, last)
            P.op("dve", lambda e: e.tensor_copy(out=MEAN[:], in_=PSUMS[:, :]), reads=[PSUMSb], writes=[MEANb])
            P.op("dve", lambda e: e.tensor_tensor(out=MSQ[:], in0=MEAN[:], in1=MEAN[:], op=ALU.mult),
                 reads=[MEANb], writes=[MSQb])
            P.op("dve", lambda e: e.tensor_tensor(out=RSTD[:], in0=PSSQ[:, :], in1=MSQ[:], op=ALU.subtract),
                 reads=[PSSQb, MSQb], writes=[RSTDb])
            P.op("act", lambda e: e.activation(out=RSTD[:], in_=RSTD[:], func=AF.Sqrt, bias=EPSC[:, 0:1], scale=1.0),
                 reads=[RSTDb, EPSCb], writes=[RSTDb])
            P.op("dve", lambda e: e.reciprocal(out=RSTD[:], in_=RSTD[:]), reads=[RSTDb], writes=[RSTDb])
            for j in range(KC):
                wz, wzb = wload(0, 16 + j)
                pz, pzb = PSR.next()
                mm_in(wz, wzb, pz, pzb)
                sz, szb = T32.next()
                P.op("act", lambda e, j=j, pz=pz, sz=sz: e.activation(
                    out=sz, in_=pz[:, :], func=AF.Silu, bias=VEC[:, V_BIN0 + 16 + j:V_BIN0 + 17 + j], scale=1.0),
                    reads=[pzb, VECb], writes=[szb])
                P.op("pool", lambda e, j=j: e.tensor_tensor(out=HC[:, j, :], in0=HC[:, j, :], in1=MEAN[:],
                                                            op=ALU.subtract), reads=[HCb[j], MEANb], writes=[HCb[j]])
                P.op("dve", lambda e, j=j: e.tensor_tensor(out=HC[:, j, :], in0=HC[:, j, :], in1=RSTD[:],
                                                           op=ALU.mult), reads=[HCb[j], RSTDb], writes=[HCb[j]])
                P.op("act", lambda e, j=j: e.activation(
                    out=HC[:, j, :], in_=HC[:, j, :], func=AF.Silu,
                    scale=VEC[:, V_LNG + j:V_LNG + j + 1], bias=VEC[:, V_LNB + j:V_LNB + j + 1]),
                    reads=[HCb[j], VECb], writes=[HCb[j]])
                P.op("dve", lambda e, j=j, sz=sz: e.tensor_tensor(out=HZ[:, j, :], in0=HC[:, j, :], in1=sz,
                                                                  op=ALU.mult), reads=[HCb[j], szb], writes=[HZb[j]])

        def mixer_b(first, last):
            if first:
                for j in range(KC):
                    P.op("pool", lambda e, j=j: e.memset(CV[:, j, 0:K1 - 1], 0.0), writes=[CVb[j]])
            for j in range(KC):
                wc, wcb = wload(1, 8 + j)
                wv, wvb = wload(1, 16 + j)
                wb_, wbb = wload(1, j)
                wz, wzb = wload(1, 24 + j)
                pcg, pcgb = PSR.next()
                pv, pvb = PSR.next()
                mm_in(wc, wcb, pcg, pcgb)
                mm_in(wv, wvb, pv, pvb)
                cg, cgb = T32.next()
                P.op("act", lambda e, pcg=pcg, cg=cg: e.activation(out=cg, in_=pcg[:, :], func=AF.Copy),
                     reads=[pcgb], writes=[cgb])
                P.op("dve", lambda e, j=j, pv=pv, cg=cg: e.tensor_tensor(
                    out=CV[:, j, K1 - 1:K1 - 1 + TOK], in0=pv[:, :], in1=cg, op=ALU.mult),
                    reads=[pvb, cgb], writes=[CVb[j]])
                pbg, pbgb = PSR.next()
                pz, pzb = PSR.next()
                mm_in(wb_, wbb, pbg, pbgb)
                mm_in(wz, wzb, pz, pzb)
                sz, szb = T32.next()
                P.op("act", lambda e, pz=pz, sz=sz: e.activation(out=sz, in_=pz[:, :], func=AF.Silu),
                     reads=[pzb], writes=[szb])
                P.op("dve", lambda e, pbg=pbg, sz=sz: e.tensor_tensor(out=sz, in0=pbg[:, :], in1=sz, op=ALU.mult),
                     reads=[pbgb, szb], writes=[szb])
                pc, pcb = PSR.next()
                for k in range(K1):
                    P.op("pe", lambda e, j=j, k=k, pc=pc: e.matmul(
                        pc[:, :], lhsT=DG1[:, j, k, :], rhs=CV[:, j, k:k + TOK], start=(k == 0), stop=(k == K1 - 1)),
                        reads=[DG1b, CVb[j]], writes=[pcb])
                P.op("dve", lambda e, j=j, pc=pc, sz=sz: e.tensor_tensor(out=HZ[:, j, :], in0=pc[:, :], in1=sz,
                                                                         op=ALU.mult),
                     reads=[pcb, szb], writes=[HZb[j]])
                if not last:
                    P.op("pool", lambda e, j=j: e.tensor_copy(out=CV[:, j, 0:K1 - 1], in_=CV[:, j, TOK:TOK + K1 - 1]),
                         reads=[CVb[j]], writes=[CVb[j]])

        def build_g(l, b):
            for half in range(2):
                pt, ptb = PSR.next()
                P.op("pe", lambda e, half=half, pt=pt: e.matmul(
                    pt[:, :], lhsT=SEL[:, b, :], rhs=GATE[l][:, half * 512:(half + 1) * 512], start=True, stop=True),
                    reads=[SELb, GATEb[l]], writes=[ptb])
                P.op("act", lambda e, half=half, pt=pt: e.activation(
                    out=G[l][:, half * 512:(half + 1) * 512], in_=pt[:, :], func=AF.Copy),
                    reads=[ptb], writes=[Gb[l]])

        def back(l, i):
            for blk in range(NB):
                for half in range(2):
                    py, pyb = PSR.next()
                    hs = slice(half * 512, (half + 1) * 512)
                    for kc in range(KC):
                        P.op("pe", lambda e, kc=kc, blk=blk, hs=hs, py=py: e.matmul(
                            py[:, :], lhsT=HZ[:, kc, blk * 128:(blk + 1) * 128], rhs=WO[l][:, kc, hs],
                            start=(kc == 0), stop=(kc == KC - 1 and l == 1)),
                            reads=[HZb[kc], WOb[l]], writes=[pyb])
                    if l == 0:
                        P.op("pe", lambda e, hs=hs, py=py: e.matmul(
                            py[:, :], lhsT=ONER[:], rhs=BOUT[:, hs], start=False, stop=True),
                            reads=[ONERb, BOUTb], writes=[pyb])
                    tm, tmb = T32.next()
                    P.op("dve", lambda e, hs=hs, py=py, tm=tm: e.tensor_tensor(
                        out=tm, in0=py[:, :], in1=G[l][:, hs], op=ALU.mult), reads=[pyb, Gb[l]], writes=[tmb])
                    P.op("pool", lambda e, blk=blk, hs=hs, tm=tm: e.tensor_tensor(
                        out=XR[i][:, blk, hs], in0=XR[i][:, blk, hs], in1=tm, op=ALU.add),
                        reads=[XRb[i][blk], tmb], writes=[XRb[i][blk]])
                st, mv, rs, smb = SM.next()
                xb = XRb[i][blk]
                for half in range(2):
                    P.op("dve", lambda e, blk=blk, half=half, st=st: e.bn_stats(
                        out=st[:, half * 6:(half + 1) * 6], in_=XR[i][:, blk, half * 512:(half + 1) * 512]),
                        reads=[xb], writes=[smb])
                P.op("dve", lambda e, st=st, mv=mv: e.bn_aggr(out=mv, in_=st), reads=[smb], writes=[smb])
                P.op("act", lambda e, mv=mv, rs=rs: e.activation(
                    out=rs, in_=mv[:, 1:2], func=AF.Sqrt, bias=EPSC[:, 1:2], scale=1.0),
                    reads=[smb, EPSCb], writes=[smb])
                P.op("dve", lambda e, rs=rs: e.reciprocal(out=rs, in_=rs), reads=[smb], writes=[smb])
                P.op("dve", lambda e, blk=blk, mv=mv, rs=rs: e.tensor_scalar(
                    out=XR[i][:, blk, :], in0=XR[i][:, blk, :], scalar1=mv[:, 0:1], scalar2=rs,
                    op0=ALU.subtract, op1=ALU.mult), reads=[xb, smb], writes=[xb])
                P.op("pool", lambda e, blk=blk: e.tensor_tensor(
                    out=XR[i][:, blk, :], in0=XR[i][:, blk, :], in1=PG[l][:], op=ALU.mult),
                    reads=[xb, PGb], writes=[xb])
                P.op("pool", lambda e, blk=blk: e.tensor_tensor(
                    out=XR[i][:, blk, :], in0=XR[i][:, blk, :], in1=PB[l][:], op=ALU.add),
                    reads=[xb, PGb], writes=[xb])

        ntiles = nseq * tps
        out_sigs = []
        load_x(0, 0)
        for t in range(ntiles):
            i = t % 2
            b = t // tps
            q = t % tps
            first, last = (q == 0), (q == tps - 1)
            if t + 1 < ntiles:
                load_x(t + 1, (t + 1) % 2)
            if first:
                for l in layers:
                    build_g(l, b)
            for l in layers:
                front(l, b, i)
                if l == 0:
                    mixer_a(first, last)
                else:
                    mixer_b(first, last)
                back(l, i)
            r0 = t * TOK
            s = P.op("sp", lambda e, r0=r0, i=i: e.dma_start(
                out=out_d[r0:r0 + TOK, :].rearrange("(b p) d -> p b d", p=128), in_=XR[i][:]),
                reads=XRb[i], dma=f"xs{i}")
            out_sigs.append(s)
        P.fence("sp", out_sigs[-2:])
        P.emit()
    return nc


def _prep_common(ada_w, ada_b, a_w_in, a_b_in, a_conv_w, a_conv_b, a_ln_g, a_ln_b, a_w_out, a_b_out,
                 b_w_in, b_conv_w, b_w_out, post_ln_g, post_ln_b, nseq):
    f = np.float32

    def units(w):
        E = w.shape[1]
        return np.ascontiguousarray(w.reshape(KC, 128, E // 128, 128).transpose(2, 1, 0, 3).reshape(E, 1024), dtype=f)

    def wout(w):
        return np.ascontiguousarray(w.reshape(KC, 128, D).transpose(1, 0, 2), dtype=f)

    def pp(v):
        return np.asarray(v, dtype=f).reshape(-1, 128).T

    vec = np.zeros((128, NV), dtype=f)
    vec[:, V_BIN0:V_BIN0 + 24] = pp(a_b_in[0])
    vec[:, V_CW0:V_CW0 + 8 * K0] = np.asarray(a_conv_w[0], dtype=f).reshape(K0, 8, 128).transpose(2, 1, 0).reshape(128, 8 * K0)
    vec[:, V_CB0:V_CB0 + 8] = pp(a_conv_b[0])
    vec[:, V_LNG:V_LNG + 8] = pp(a_ln_g[0])
    vec[:, V_LNB:V_LNB + 8] = pp(a_ln_b[0])
    vec[:, V_CW1:V_CW1 + 8 * K1] = np.asarray(b_conv_w[0], dtype=f).reshape(K1, 8, 128).transpose(2, 1, 0).reshape(128, 8 * K1)
    sel = np.zeros((nseq, nseq, 128), dtype=f)
    for b in range(nseq):
        sel[b, b, :] = 1.0 / ALPHA
    return {
        "ada_w": np.ascontiguousarray(ada_w, dtype=f),
        "ada_b4": np.ascontiguousarray(np.broadcast_to(np.asarray(ada_b, dtype=f)[:, None, :], (2, nseq, 3 * D))),
        "win0": units(np.asarray(a_w_in[0])),
        "win1": units(np.asarray(b_w_in[0])),
        "wout0": wout(np.asarray(a_w_out[0])),
        "wout1": wout(np.asarray(b_w_out[0])),
        "vecs": vec,
        "bout0": np.ascontiguousarray(np.asarray(a_b_out, dtype=f).reshape(1, D)),
        "pg": np.ascontiguousarray(np.broadcast_to(np.asarray(post_ln_g, dtype=f)[:, None, :], (2, 128, D))),
        "pb": np.ascontiguousarray(np.broadcast_to(np.asarray(post_ln_b, dtype=f)[:, None, :], (2, 128, D))),
        "ident": np.eye(128, dtype=f),
        "sel": sel,
    }


def _core_maps(x, c, common, n_cores, nseq):
    seq = x.shape[1]
    maps = []
    for i in range(n_cores):
        xs = np.ascontiguousarray(x[i * nseq:(i + 1) * nseq].reshape(nseq * seq, D), dtype=np.float32)
        cs = np.asarray(c[i * nseq:(i + 1) * nseq], dtype=np.float32)
        cT = np.ascontiguousarray(cs.reshape(nseq, KC, 128).transpose(2, 1, 0))
        m = dict(common)
        m["x"] = xs
        m["cT"] = cT
        maps.append(m)
    return maps


def run(x, c, params, n_cores, layers=(0, 1)):
    B, S, _ = x.shape
    nseq = B // n_cores
    common = _prep_common(nseq=nseq, **params)
    maps = _core_maps(x, c, common, n_cores, nseq)
    nc = build(nseq, S, layers=layers)
    res = run_bass_kernel_spmd(nc, maps, core_ids=list(range(n_cores)))
    outs = [np.asarray(r["out"]).reshape(nseq, S, D) for r in res.results]
    return np.concatenate(outs, axis=0).astype(np.float32)


def kernel(x, c, ada_w, ada_b, a_w_in, a_b_in, a_conv_w, a_conv_b, a_ln_g, a_ln_b,
           a_w_out, a_b_out, b_w_in, b_conv_w, b_w_out, post_ln_g, post_ln_b):
    params = dict(ada_w=np.asarray(ada_w), ada_b=np.asarray(ada_b), a_w_in=np.asarray(a_w_in),
                  a_b_in=np.asarray(a_b_in), a_conv_w=np.asarray(a_conv_w), a_conv_b=np.asarray(a_conv_b),
                  a_ln_g=np.asarray(a_ln_g), a_ln_b=np.asarray(a_ln_b), a_w_out=np.asarray(a_w_out),
                  a_b_out=np.asarray(a_b_out), b_w_in=np.asarray(b_w_in), b_conv_w=np.asarray(b_conv_w),
                  b_w_out=np.asarray(b_w_out), post_ln_g=np.asarray(post_ln_g), post_ln_b=np.asarray(post_ln_b))
    return run(np.asarray(x), np.asarray(c), params, n_cores=8)
```
